# Optimizing a Trainium2 kernel written in Bass

```python
import math
import jax, jax.numpy as jnp
from jax import lax
import numpy as np

D_MODEL = 4096
BATCH = 4
SEQ = 4096
DEPTH = 2
DEC_BATCH = 16
DEC_SEQ = 64
PAST_LEN = 2048

CHUNK = 64
PE_DIM = 256
D_FF = 4 * D_MODEL
LN_EPS = 1e-5
RMS_EPS = 1e-6
NEG_INF = -1e30

SWA_HEADS = 32
SWA_KV_HEADS = 4
SWA_GROUP = SWA_HEADS // SWA_KV_HEADS
SWA_HEAD_DIM = 64
SWA_WINDOW = 128
SWA_BACK_CHUNKS = SWA_WINDOW // CHUNK
GLA_HEADS = 4
GLA_HEAD_K = 256
GLA_HEAD_V = 512
GLA_RANK = 16
GLA_TAU = 16.0
GDN_HEADS = 16
GDN_HEAD_K = 128
GDN_HEAD_V = 128
GDN_CONV = 4

SWA_Q_W = SWA_HEADS * SWA_HEAD_DIM
SWA_KV_W = SWA_KV_HEADS * SWA_HEAD_DIM
GLA_K_W = GLA_HEADS * GLA_HEAD_K
GLA_V_W = GLA_HEADS * GLA_HEAD_V
GDN_K_W = GDN_HEADS * GDN_HEAD_K
GDN_V_W = GDN_HEADS * GDN_HEAD_V
GDN_CONV_CH = 2 * GDN_K_W + GDN_V_W
N_BRANCH = 3
IN_SIZES = (SWA_Q_W, SWA_KV_W, SWA_KV_W, GLA_K_W, GLA_K_W, GLA_V_W, GLA_RANK, GLA_V_W,
            GDN_CONV_CH, GDN_HEADS, GDN_HEADS, GDN_V_W, N_BRANCH * D_MODEL)
IN_COLS = sum(IN_SIZES)

DN_ALPHA = (2 * DEPTH) ** 0.25
DN_BETA = (8 * DEPTH) ** -0.25

kernel_name = 'hybrid_swa_gla_gdn_stream_step'


def _split_points():
    return tuple(int(c) for c in np.cumsum(IN_SIZES)[:-1])


def layer_norm(x, g, b):
    xf = x.astype(jnp.float32)
    mu = jnp.mean(xf, -1, keepdims=True)
    var = jnp.mean(jnp.square(xf - mu), -1, keepdims=True)
    return ((xf - mu) * lax.rsqrt(var + LN_EPS) * g.astype(jnp.float32) + b.astype(jnp.float32)).astype(x.dtype)


def rms_norm(x, g):
    xf = x.astype(jnp.float32)
    return xf * lax.rsqrt(jnp.mean(xf * xf, -1, keepdims=True) + RMS_EPS) * g.astype(jnp.float32)


def l2_normalize(x):
    return x * lax.rsqrt(jnp.sum(x * x, -1, keepdims=True) + 1e-6)


def alibi_slopes():
    return 2.0 ** (-8.0 * jnp.arange(1, SWA_HEADS + 1, dtype=jnp.float32) / SWA_HEADS)


def _to_blocks(t, blk):
    B, L = t.shape[:2]
    return jnp.moveaxis(t.reshape(B, L // blk, blk, *t.shape[2:]), 1, 0)


def _from_blocks(o):
    n, B, blk = o.shape[:3]
    return jnp.moveaxis(o, 0, 1).reshape(B, n * blk, *o.shape[3:])


def sink_attention(q, k, v, dist, valid, sinks):
    f32 = jnp.float32
    s = jnp.einsum('bnqkgd,bnskd->bnkgqs', q, k).astype(f32) * (SWA_HEAD_DIM ** -0.5)
    s = s - alibi_slopes().reshape(SWA_KV_HEADS, SWA_GROUP, 1, 1) * dist
    if valid is not None:
        s = jnp.where(valid[None, :, None, None], s, NEG_INF)
    sink = jnp.broadcast_to(sinks.astype(f32).reshape(SWA_KV_HEADS, SWA_GROUP, 1, 1), s.shape[:-1] + (1,))
    p = jax.nn.softmax(jnp.concatenate([s, sink], axis=-1), axis=-1)[..., :-1]
    return jnp.einsum('bnkgqs,bnskd->bnqkgd', p.astype(v.dtype), v)


def swa_prompt(q, k, v, sinks):
    B, S = q.shape[:2]
    n = S // CHUNK
    back = SWA_BACK_CHUNKS
    pad = ((0, 0), (back * CHUNK, 0), (0, 0), (0, 0))
    kc = jnp.pad(k, pad).reshape(B, n + back, CHUNK, SWA_KV_HEADS, SWA_HEAD_DIM)
    vc = jnp.pad(v, pad).reshape(B, n + back, CHUNK, SWA_KV_HEADS, SWA_HEAD_DIM)
    kb = jnp.concatenate([kc[:, j:j + n] for j in range(back + 1)], axis=2)
    vb = jnp.concatenate([vc[:, j:j + n] for j in range(back + 1)], axis=2)
    qb = q.reshape(B, n, CHUNK, SWA_KV_HEADS, SWA_GROUP, SWA_HEAD_DIM)
    qpos = jnp.arange(CHUNK) + back * CHUNK
    kpos = jnp.arange((back + 1) * CHUNK)
    dist = jnp.abs(qpos[:, None] - kpos[None, :]).astype(jnp.float32)
    valid = (jnp.arange(n)[:, None] * CHUNK + kpos[None, :] >= back * CHUNK)[:, None, :]
    o = sink_attention(qb, kb, vb, dist, valid, sinks)
    return o.reshape(B, S, SWA_Q_W)


def swa_sample(q, k, v, k_cache, v_cache, sinks):
    B, L = q.shape[:2]
    W = k_cache.shape[1]
    kk = jnp.concatenate([k_cache.astype(k.dtype), k], axis=1)
    vv = jnp.concatenate([v_cache.astype(v.dtype), v], axis=1)
    qb = q.reshape(B, 1, L, SWA_KV_HEADS, SWA_GROUP, SWA_HEAD_DIM)
    dist = jnp.abs((jnp.arange(L) + W)[:, None] - jnp.arange(W + L)[None, :]).astype(jnp.float32)
    o = sink_attention(qb, kk[:, None], vv[:, None], dist, None, sinks)
    return o.reshape(B, L, SWA_Q_W), kk[:, L:], vv[:, L:]


def gla_scan(q, k, v, log_a, s0):
    L = q.shape[1]
    blk = min(CHUNK, L)
    causal = jnp.tril(jnp.ones((blk, blk), bool))

    def step(S, inp):
        qb, kb, vb, gb = inp
        b = jnp.cumsum(gb, axis=1)
        qd = qb * jnp.exp(b)
        kd = kb * jnp.exp(-b)
        att = jnp.where(causal, jnp.einsum('bthk,bshk->bhts', qd, kd), 0.0)
        o = jnp.einsum('bhts,bshv->bthv', att, vb) + jnp.einsum('bthk,bhkv->bthv', qd, S)
        bl = b[:, -1]
        S = S * jnp.exp(bl)[..., None] + jnp.einsum('bshk,bshv->bhkv', kb * jnp.exp(bl[:, None] - b), vb)
        return S, o

    S, o = lax.scan(step, s0, (_to_blocks(q, blk), _to_blocks(k, blk), _to_blocks(v, blk), _to_blocks(log_a, blk)))
    return _from_blocks(o), S


def gdn_scan(q, k, v, g, beta, s0):
    L = q.shape[1]
    blk = min(CHUNK, L)
    eye = jnp.eye(blk, dtype=jnp.float32)
    lower = jnp.tril(jnp.ones((blk, blk), bool))
    strict = jnp.tril(jnp.ones((blk, blk), bool), -1)

    def step(S, inp):
        qb, kb, vb, gb, bb = inp
        gc = jnp.cumsum(gb, axis=1)
        gh = jnp.swapaxes(gc, 1, 2)
        decay = jnp.exp(jnp.where(lower, gh[..., :, None] - gh[..., None, :], -jnp.inf))
        bh = jnp.swapaxes(bb, 1, 2)
        kk = jnp.einsum('bthk,bshk->bhts', kb, kb)
        a = kk * jnp.where(strict, decay, 0.0) * bh[..., :, None]
        T = lax.linalg.triangular_solve(eye + a, jnp.broadcast_to(eye, a.shape), left_side=True, lower=True)
        u = jnp.einsum('bhts,bshv->bthv', T, vb * bb[..., None])
        w = jnp.einsum('bhts,bshk->bthk', T, kb * (bb * jnp.exp(gc))[..., None])
        v_new = u - jnp.einsum('bthk,bhkv->bthv', w, S)
        qk = jnp.einsum('bthk,bshk->bhts', qb, kb) * decay
        o = jnp.einsum('bthk,bhkv->bthv', qb * jnp.exp(gc)[..., None], S) + jnp.einsum('bhts,bshv->bthv', qk, v_new)
        gl = gc[:, -1]
        S = S * jnp.exp(gl)[..., None, None] + jnp.einsum('bshk,bshv->bhkv', kb * jnp.exp(gl[:, None] - gc)[..., None], v_new)
        return S, o

    S, o = lax.scan(step, s0, (_to_blocks(q, blk), _to_blocks(k, blk), _to_blocks(v, blk), _to_blocks(g, blk), _to_blocks(beta, blk)))
    return _from_blocks(o), S


def gla_mixer(q, k, v, lr, r, w_gate2, gate_bias, norm_w, s0):
    f32 = jnp.float32
    B, L = q.shape[:2]
    q = q.astype(f32).reshape(B, L, GLA_HEADS, GLA_HEAD_K) * (GLA_HEAD_K ** -0.5)
    k = k.astype(f32).reshape(B, L, GLA_HEADS, GLA_HEAD_K)
    v = v.astype(f32).reshape(B, L, GLA_HEADS, GLA_HEAD_V)
    log_a = jax.nn.log_sigmoid(lr.astype(f32) @ w_gate2.astype(f32) + gate_bias.astype(f32)) / GLA_TAU
    o, S = gla_scan(q, k, v, log_a.reshape(B, L, GLA_HEADS, GLA_HEAD_K), s0.astype(f32))
    o = rms_norm(o, norm_w) * jax.nn.silu(r.astype(f32).reshape(B, L, GLA_HEADS, GLA_HEAD_V))
    return o.reshape(B, L, GLA_V_W), S


def gdn_mixer(qkv, b_raw, a_raw, z, conv_buf, conv_w, a_log, dt_bias, norm_w, s0):
    f32 = jnp.float32
    B, L = qkv.shape[:2]
    xc = jnp.concatenate([conv_buf.astype(qkv.dtype), qkv], axis=1)
    new_buf = xc[:, L:]
    conv = lax.conv_general_dilated(xc.astype(f32), conv_w.astype(f32).reshape(GDN_CONV, 1, GDN_CONV_CH),
                                    window_strides=(1,), padding='VALID',
                                    dimension_numbers=('NWC', 'WIO', 'NWC'), feature_group_count=GDN_CONV_CH)
    conv = jax.nn.silu(conv)
    q, k, v = jnp.split(conv, (GDN_K_W, 2 * GDN_K_W), axis=-1)
    q = l2_normalize(q.reshape(B, L, GDN_HEADS, GDN_HEAD_K)) * (GDN_HEAD_K ** -0.5)
    k = l2_normalize(k.reshape(B, L, GDN_HEADS, GDN_HEAD_K))
    v = v.reshape(B, L, GDN_HEADS, GDN_HEAD_V)
    beta = jax.nn.sigmoid(b_raw.astype(f32))
    g = -jnp.exp(a_log.astype(f32)) * jax.nn.softplus(a_raw.astype(f32) + dt_bias.astype(f32))
    o, S = gdn_scan(q, k, v, g, beta, s0.astype(f32))
    o = rms_norm(o, norm_w) * jax.nn.silu(z.astype(f32).reshape(B, L, GDN_HEADS, GDN_HEAD_V))
    return o.reshape(B, L, GDN_V_W), S, new_buf


def trunk_layer(x, pe, W, swa_cache, gla_s0, gdn_s0, conv_buf):
    B, L, _ = x.shape
    dt = x.dtype
    proj = x @ W['w_in']
    (sq, sk, sv, gq, gk, gv, glr, gr, dqkv, db, da, dz, mg) = jnp.split(proj, _split_points(), axis=-1)
    sq = sq.reshape(B, L, SWA_HEADS, SWA_HEAD_DIM)
    sk = sk.reshape(B, L, SWA_KV_HEADS, SWA_HEAD_DIM)
    sv = sv.reshape(B, L, SWA_KV_HEADS, SWA_HEAD_DIM)
    if swa_cache is None:
        o_a = swa_prompt(sq, sk, sv, W['swa_sinks'])
        n_keep = min(SWA_WINDOW, L)
        new_k, new_v = sk[:, L - n_keep:], sv[:, L - n_keep:]
    else:
        o_a, new_k, new_v = swa_sample(sq, sk, sv, swa_cache[0], swa_cache[1], W['swa_sinks'])
    o_b, gla_s = gla_mixer(gq, gk, gv, glr, gr, W['gla_w_gate2'], W['gla_gate_bias'], W['gla_norm_w'], gla_s0)
    o_c, gdn_s, new_buf = gdn_mixer(dqkv, db, da, dz, conv_buf, W['gdn_conv_w'], W['gdn_a_log'],
                                    W['gdn_dt_bias'], W['gdn_norm_w'], gdn_s0)
    gates = jax.nn.sigmoid(mg.astype(jnp.float32)).reshape(B, L, N_BRANCH, D_MODEL)
    merged = (gates[:, :, 0] * (o_a.astype(dt) @ W['w_br_swa'])
              + gates[:, :, 1] * (o_b.astype(dt) @ W['w_br_gla'])
              + gates[:, :, 2] * (o_c.astype(dt) @ W['w_br_gdn']))
    x = layer_norm(DN_ALPHA * x + merged.astype(dt) @ W['w_out'], W['ln1_g'], W['ln1_b'])
    x = layer_norm(DN_ALPHA * x + jnp.square(jax.nn.relu(x @ W['w_up'])) @ W['w_down'], W['ln2_g'], W['ln2_b'])
    x = layer_norm(DN_ALPHA * x + jax.nn.sigmoid(x @ W['pe_w_gate']) * (pe @ W['pe_w_proj']), W['ln3_g'], W['ln3_b'])
    return x, (new_k, new_v, gla_s, gdn_s, new_buf)


def setup_inputs(seed: int = 0) -> dict:
    key = jax.random.key(seed)
    ks = iter(jax.random.split(key, 48))

    def nrm(shape, scale):
        return jax.random.normal(next(ks), shape, jnp.float32) * scale

    n_win = min(SWA_WINDOW, PAST_LEN)
    dt0 = jnp.exp(jax.random.uniform(next(ks), (DEPTH, GDN_HEADS), jnp.float32, math.log(1e-3), math.log(1e-1)))
    return {
        'x_prompt': nrm((BATCH, SEQ, D_MODEL), 1.0),
        'x_sample': nrm((DEC_BATCH, DEC_SEQ, D_MODEL), 1.0),
        'cache_swa_k': nrm((DEPTH, DEC_BATCH, n_win, SWA_KV_HEADS, SWA_HEAD_DIM), 1.0),
        'cache_swa_v': nrm((DEPTH, DEC_BATCH, n_win, SWA_KV_HEADS, SWA_HEAD_DIM), 1.0),
        'state_gla': nrm((DEPTH, DEC_BATCH, GLA_HEADS, GLA_HEAD_K, GLA_HEAD_V), 0.1),
        'state_gdn': nrm((DEPTH, DEC_BATCH, GDN_HEADS, GDN_HEAD_K, GDN_HEAD_V), 0.1),
        'state_gdn_conv': nrm((DEPTH, DEC_BATCH, GDN_CONV - 1, GDN_CONV_CH), 1.0),
        'p_prompt': nrm((DEPTH, BATCH, SEQ, PE_DIM), 1.0),
        'p_sample': nrm((DEPTH, DEC_BATCH, DEC_SEQ, PE_DIM), 1.0),
        'w_in': nrm((DEPTH, D_MODEL, IN_COLS), D_MODEL ** -0.5),
        'swa_sinks': nrm((DEPTH, SWA_HEADS), 1.0),
        'gla_w_gate2': nrm((DEPTH, GLA_RANK, GLA_K_W), GLA_RANK ** -0.5),
        'gla_gate_bias': nrm((DEPTH, GLA_K_W), 0.1),
        'gla_norm_w': 1.0 + nrm((DEPTH, GLA_HEAD_V), 0.02),
        'gdn_conv_w': nrm((DEPTH, GDN_CONV, GDN_CONV_CH), GDN_CONV ** -0.5),
        'gdn_a_log': jnp.log(jax.random.uniform(next(ks), (DEPTH, GDN_HEADS), jnp.float32, 1.0, 16.0)),
        'gdn_dt_bias': dt0 + jnp.log(-jnp.expm1(-dt0)),
        'gdn_norm_w': 1.0 + nrm((DEPTH, GDN_HEAD_V), 0.02),
        'w_br_swa': nrm((DEPTH, SWA_Q_W, D_MODEL), SWA_Q_W ** -0.5),
        'w_br_gla': nrm((DEPTH, GLA_V_W, D_MODEL), GLA_V_W ** -0.5),
        'w_br_gdn': nrm((DEPTH, GDN_V_W, D_MODEL), GDN_V_W ** -0.5),
        'w_out': nrm((DEPTH, D_MODEL, D_MODEL), DN_BETA * D_MODEL ** -0.5),
        'ln1_g': 1.0 + nrm((DEPTH, D_MODEL), 0.02),
        'ln1_b': nrm((DEPTH, D_MODEL), 0.02),
        'w_up': nrm((DEPTH, D_MODEL, D_FF), D_MODEL ** -0.5),
        'w_down': nrm((DEPTH, D_FF, D_MODEL), DN_BETA * D_FF ** -0.5),
        'ln2_g': 1.0 + nrm((DEPTH, D_MODEL), 0.02),
        'ln2_b': nrm((DEPTH, D_MODEL), 0.02),
        'pe_w_gate': nrm((DEPTH, D_MODEL, D_MODEL), D_MODEL ** -0.5),
        'pe_w_proj': nrm((DEPTH, PE_DIM, D_MODEL), DN_BETA * PE_DIM ** -0.5),
        'ln3_g': 1.0 + nrm((DEPTH, D_MODEL), 0.02),
        'ln3_b': nrm((DEPTH, D_MODEL), 0.02),
    }


def reference(x_prompt, x_sample, cache_swa_k, cache_swa_v, state_gla, state_gdn, state_gdn_conv,
              p_prompt, p_sample, w_in, swa_sinks, gla_w_gate2, gla_gate_bias, gla_norm_w,
              gdn_conv_w, gdn_a_log, gdn_dt_bias, gdn_norm_w, w_br_swa, w_br_gla, w_br_gdn, w_out,
              ln1_g, ln1_b, w_up, w_down, ln2_g, ln2_b, pe_w_gate, pe_w_proj, ln3_g, ln3_b):
    f32 = jnp.float32
    bp = x_prompt.shape[0]
    yp, ys = x_prompt, x_sample
    st_p, st_s = [], []
    for l in range(DEPTH):
        W = {'w_in': w_in[l], 'swa_sinks': swa_sinks[l], 'gla_w_gate2': gla_w_gate2[l],
             'gla_gate_bias': gla_gate_bias[l], 'gla_norm_w': gla_norm_w[l], 'gdn_conv_w': gdn_conv_w[l],
             'gdn_a_log': gdn_a_log[l], 'gdn_dt_bias': gdn_dt_bias[l], 'gdn_norm_w': gdn_norm_w[l],
             'w_br_swa': w_br_swa[l], 'w_br_gla': w_br_gla[l], 'w_br_gdn': w_br_gdn[l], 'w_out': w_out[l],
             'ln1_g': ln1_g[l], 'ln1_b': ln1_b[l], 'w_up': w_up[l], 'w_down': w_down[l],
             'ln2_g': ln2_g[l], 'ln2_b': ln2_b[l], 'pe_w_gate': pe_w_gate[l], 'pe_w_proj': pe_w_proj[l],
             'ln3_g': ln3_g[l], 'ln3_b': ln3_b[l]}
        yp, sp = trunk_layer(yp, p_prompt[l], W, None,
                             jnp.zeros((bp, GLA_HEADS, GLA_HEAD_K, GLA_HEAD_V), f32),
                             jnp.zeros((bp, GDN_HEADS, GDN_HEAD_K, GDN_HEAD_V), f32),
                             jnp.zeros((bp, GDN_CONV - 1, GDN_CONV_CH), x_prompt.dtype))
        ys, ss = trunk_layer(ys, p_sample[l], W, (cache_swa_k[l], cache_swa_v[l]),
                             state_gla[l], state_gdn[l], state_gdn_conv[l])
        st_p.append(sp)
        st_s.append(ss)
    swa_k_prompt = jnp.stack([s[0] for s in st_p])
    swa_v_prompt = jnp.stack([s[1] for s in st_p])
    gla_prompt = jnp.stack([s[2] for s in st_p])
    gdn_prompt = jnp.stack([s[3] for s in st_p])
    gdn_conv_prompt = jnp.stack([s[4] for s in st_p])
    swa_k_sample = jnp.stack([s[0] for s in st_s])
    swa_v_sample = jnp.stack([s[1] for s in st_s])
    gla_sample = jnp.stack([s[2] for s in st_s])
    gdn_sample = jnp.stack([s[3] for s in st_s])
    gdn_conv_sample = jnp.stack([s[4] for s in st_s])
    return (yp, ys, swa_k_prompt, swa_v_prompt, gla_prompt, gdn_prompt, gdn_conv_prompt,
            swa_k_sample, swa_v_sample, gla_sample, gdn_sample, gdn_conv_sample)
```

```python
import math
from contextlib import ExitStack
import numpy as np
import concourse.bass as bass
import concourse.mybir as mybir
from concourse.bass_utils import run_bass_kernel_spmd

F32, BF16 = mybir.dt.float32, mybir.dt.bfloat16
ALU = mybir.AluOpType
AF = mybir.ActivationFunctionType
DEBUG_NAMES = None
STOP_AT = None


class StopBuild(Exception):
    pass


def ckpt(name):
    if STOP_AT == name:
        raise StopBuild()
CH = 64
T = 256
NCH = T // CH


class Cfg:
    def __init__(s, D=4096, SEQ=4096, DFF=16384, DEPTH=2, SWA_H=32, SWA_KV=4, GLA_H=4, GDN_H=16,
                 NPROMPT=4, NSAMPLE=16, NCORES=4, PE_DIM=256):
        s.D, s.SEQ, s.DFF, s.DEPTH = D, SEQ, DFF, DEPTH
        s.SWA_H, s.SWA_KV, s.GLA_H, s.GDN_H = SWA_H, SWA_KV, GLA_H, GDN_H
        s.GRP = SWA_H // SWA_KV
        s.NPROMPT, s.NSAMPLE, s.NCORES, s.PE_DIM = NPROMPT, NSAMPLE, NCORES, PE_DIM
        s.SQ, s.SKV = SWA_H * 64, SWA_KV * 64
        s.GK, s.GV = GLA_H * 256, GLA_H * 512
        s.DK, s.DV = GDN_H * 128, GDN_H * 128
        s.CONVC = 2 * s.DK + s.DV
        sizes = [s.SQ, s.SKV, s.SKV, s.GK, s.GK, s.GV, 16, s.GV, s.CONVC, GDN_H, GDN_H, s.DV, 3 * D]
        off = np.concatenate([[0], np.cumsum(sizes)]).astype(int)
        (s.o_sq, s.o_sk, s.o_sv, s.o_gq, s.o_gk, s.o_gv, s.o_glr, s.o_gr, s.o_dqkv, s.o_db, s.o_da,
         s.o_dz, s.o_mg) = [int(x) for x in off[:-1]]
        s.INC = int(off[-1])
        s.ALPHA = (2 * DEPTH) ** 0.25
        s.PPC = NPROMPT // NCORES
        s.SPC = NSAMPLE // NCORES
        assert s.SPC % NCH == 0 and SEQ % T == 0
        s.PT = SEQ // T
        s.NT = s.PPC * s.PT + s.SPC // NCH
        s.NTOK = s.NT * T
        s.NSEQ = s.PPC + s.SPC
        s.KC = D // 128


class V:
    def __init__(s, ap, res):
        s.ap = ap
        s.res = res if isinstance(res, (list, tuple)) else [res]


class Prog:
    def __init__(s, nc, es):
        s.nc, s.es = nc, es
        s.eng = {'pe': nc.tensor, 'act': nc.scalar, 'dve': nc.vector, 'pool': nc.gpsimd, 'sp': nc.sync}
        s.semh, s.cnt = {}, {}
        s.known = {e: {} for e in s.eng}
        s.lastw, s.readers = {}, {}
        s.nins = 0
        for e in ('pe', 'act', 'dve', 'pool'):
            s._sem(e)

    def _sem(s, k):
        if k not in s.semh:
            s.semh[k] = s.es.enter_context(s.nc.semaphore('s_' + k.replace('.', '_').replace(':', '_')))
            s.cnt[k] = 0
        return s.semh[k]

    def _waits(s, e, reads, writes):
        need = {}

        def add(t):
            if t is None:
                return
            k, v = t
            if e == 'pe' and k == 'pe':
                return
            if need.get(k, 0) < v:
                need[k] = v
        for r in reads:
            add(s.lastw.get(r))
        for w in writes:
            add(s.lastw.get(w))
            for t in s.readers.get(w, ()):
                add(t)
        for k, v in need.items():
            if s.known[e].get(k, 0) >= v:
                continue
            s.eng[e].wait_ge(s.semh[k], v)
            s.known[e][k] = v

    def _commit(s, ticket, reads, writes):
        for r in reads:
            if r not in writes:
                s.readers.setdefault(r, []).append(ticket)
        for w in writes:
            s.lastw[w] = ticket
            s.readers[w] = []

    def op(s, e, method, outs=('out',), **kw):
        reads, writes, args = [], [], {}
        for k, v in kw.items():
            if isinstance(v, V):
                args[k] = v.ap
                (writes if k in outs or k == 'accum_out' else reads).extend(v.res)
            else:
                args[k] = v
        if e != 'pe':
            for r in reads:
                if r.startswith('ps') and r not in writes:
                    writes.append(r)
        s._waits(e, reads, writes)
        ins = getattr(s.eng[e], method)(**args)
        if DEBUG_NAMES is not None:
            DEBUG_NAMES[ins.ins.name] = (e, method, {k: (v if not hasattr(v, 'shape') else tuple(v.shape)) for k, v in args.items()})
        s.cnt[e] += 1
        ins.then_inc(s.semh[e], 1)
        s._commit((e, s.cnt[e]), reads, writes)
        s.nins += 1
        return ins

    def dma(s, q, out, in_, **kw):
        chan = 'd.' + (out.res[0] if not out.res[0].startswith('dram') else in_.res[0])
        s._sem(chan)
        s._waits(q, in_.res, out.res)
        ins = s.eng[q].dma_start(out=out.ap, in_=in_.ap, **kw)
        s.cnt[chan] += 16
        ins.then_inc(s.semh[chan], 16)
        s._commit((chan, s.cnt[chan]), in_.res, out.res)
        s.nins += 1

    def barrier(s):
        for e in s.eng:
            for k, v in s.cnt.items():
                if v > 0 and k != e and s.known[e].get(k, 0) < v:
                    s.eng[e].wait_ge(s.semh[k], v)
                    s.known[e][k] = v

    def finish(s, q):
        for k, v in s.cnt.items():
            if k.startswith('d.') and v > 0 and s.known[q].get(k, 0) < v:
                s.eng[q].wait_ge(s.semh[k], v)
        for e in ('pe', 'act', 'dve', 'pool'):
            if e != q and s.cnt[e] > 0:
                s.eng[q].wait_ge(s.semh[e], s.cnt[e])


def make_consts():
    c = np.zeros((128, 128 + 128 + 64 * 5), np.float32)
    c[:, 0:128] = np.eye(128)
    c[:, 128:256] = 1.0
    i = np.arange(64)
    c[:64, 256:320] = (i[:, None] <= i[None, :])
    c[:64, 320:384] = (i[:, None] < i[None, :])
    for j in range(3):
        c[:64, 384 + 64 * j:448 + 64 * j] = np.abs(i[None, :] + 128 - (64 * j + i[:, None]))
    return c


def build(cfg):
    nc = bass.Bass("TRN2", target_bir_lowering=False)
    es = ExitStack()
    D, KC, L = cfg.D, cfg.KC, cfg.DEPTH

    def din(name, shape):
        return nc.dram_tensor(name, list(shape), F32, kind="ExternalInput").ap()

    def dout(name, shape):
        return nc.dram_tensor(name, list(shape), F32, kind="ExternalOutput").ap()
    xin = din("xin", [cfg.NTOK, D])
    pin = din("pin", [L, cfg.NTOK, cfg.PE_DIM])
    ck_in = din("ck_in", [L, cfg.SPC, 128, cfg.SKV])
    cv_in = din("cv_in", [L, cfg.SPC, 128, cfg.SKV])
    gla_in = din("gla_in", [L, cfg.SPC, cfg.GK, 512])
    gdn_in = din("gdn_in", [L, cfg.SPC, cfg.DK, 128])
    conv_in = din("conv_in", [L, cfg.SPC, 3, cfg.CONVC])
    consts = din("consts", [128, 576])
    w_in = din("w_in", [L, D, cfg.INC])
    sinks = din("swa_sinks", [L, cfg.SWA_H])
    w_gate2 = din("gla_w_gate2", [L, 16, cfg.GK])
    gate_bias = din("gla_gate_bias", [L, cfg.GK])
    gla_nw = din("gla_norm_w", [L, 512])
    conv_w = din("gdn_conv_w", [L, 4, cfg.CONVC])
    a_log = din("gdn_a_log", [L, cfg.GDN_H])
    dt_bias = din("gdn_dt_bias", [L, cfg.GDN_H])
    gdn_nw = din("gdn_norm_w", [L, 128])
    w_br = [din("w_br_swa", [L, cfg.SQ, D]), din("w_br_gla", [L, cfg.GV, D]), din("w_br_gdn", [L, cfg.DV, D])]
    w_out = din("w_out", [L, D, D])
    lng = [din("ln%d_g" % i, [L, D]) for i in (1, 2, 3)]
    lnb = [din("ln%d_b" % i, [L, D]) for i in (1, 2, 3)]
    w_up = din("w_up", [L, D, cfg.DFF])
    w_down = din("w_down", [L, cfg.DFF, D])
    pe_g = din("pe_w_gate", [L, D, D])
    pe_p = din("pe_w_proj", [L, cfg.PE_DIM, D])
    y = dout("y", [cfg.NTOK, D])
    o_k = dout("o_swa_k", [L, cfg.NSEQ, 128, cfg.SKV])
    o_v = dout("o_swa_v", [L, cfg.NSEQ, 128, cfg.SKV])
    o_gla = dout("o_gla", [L, cfg.NSEQ, cfg.GK, 512])
    o_gdn = dout("o_gdn", [L, cfg.NSEQ, cfg.DK, 128])
    o_conv = dout("o_conv", [L, cfg.NSEQ, 3, cfg.CONVC])
    scratch = nc.dram_tensor("scratch", [cfg.NT, 128, KC * T], F32).ap()

    P = Prog(nc, es)

    def sb(name, shape, dt=F32):
        return es.enter_context(nc.sbuf_tensor(name, list(shape), dt))
    X = sb("X", [128, KC, T])
    xb = sb("xb", [128, KC, T], BF16)
    oTa = sb("oTa", [64, cfg.SWA_H, T], BF16)
    oTb = sb("oTb", [128, cfg.GV // 128, T], BF16)
    oTc = sb("oTc", [128, cfg.GDN_H, T], BF16)
    mb = sb("mb", [128, KC, T], BF16)
    WS = 8192
    wsl = [sb("wsl%d" % i, [128, WS], BF16) for i in range(2)]
    cst = sb("cst", [128, 576])
    cstb = sb("cstb", [128, 576], BF16)
    I32, ONES32 = cst[:, 0:128], cst[:, 128:256]
    Ibf, ONESbf = cstb[:, 0:128], cstb[:, 128:256]
    RC = ['cst']
    psums = [es.enter_context(nc.psum_tensor("ps%d" % i, [128, 512], F32)) for i in range(8)]
    pidx = [0]

    def ps():
        i = pidx[0] % 8
        pidx[0] += 1
        return psums[i], 'ps%d' % i
    widx = [0]

    wctx = {'ti': 0, 'idx': 0}
    wscr = {}

    def wload(src_ap, nk, ncols, prow=128):
        i = widx[0] % 2
        widx[0] += 1
        n = nk * ncols
        assert n <= WS
        t = wsl[i]
        flat = t[0:prow, 0:n]
        dst = flat.rearrange("p (k c) -> p k c", k=nk)
        bi = wctx['idx']
        wctx['idx'] += 1
        names = ['wsl%d.p%d' % (i, pi) for pi in range(4)]
        if wctx['ti'] == 0:
            srcv = src_ap.rearrange("(k p) c -> p k c", p=prow)
            nparts = 4 if nk >= 16 else 1
            for pi in range(nparts):
                k0, k1 = pi * nk // nparts, (pi + 1) * nk // nparts
                P.dma('pool', out=V(dst[:, k0:k1, :], names[pi]), in_=V(srcv[:, k0:k1, :], 'dramw'))
            if bi not in wscr:
                wscr[bi] = nc.dram_tensor("ws%d" % bi, [prow, n], BF16).ap()
            P.dma('sp', out=V(wscr[bi][:, :], 'dram_ws%d' % bi), in_=V(flat, names))
        else:
            P.dma('pool' if i == 0 else 'sp', out=V(flat, names), in_=V(wscr[bi][:, :], 'dram_ws%d' % bi))
        return dst, names

    P.dma('sp', out=V(cst[:, :], 'cst'), in_=V(consts[:, :], 'dramc'))
    P.op('dve', 'tensor_copy', out=V(cstb[:, :], 'cst2'), in_=V(cst[:, :], 'cst'))
    RC = ['cst', 'cst2']
    U32, Us32 = cst[0:64, 256:320], cst[0:64, 320:384]
    Ubf = cstb[0:64, 256:320]
    Dj = [cst[0:64, 384 + 64 * j:448 + 64 * j] for j in range(3)]
    slopes = [2.0 ** (-8.0 * (h + 1) / cfg.SWA_H) for h in range(cfg.SWA_H)]

    lnp = sb("lnp", [128, 6, KC])
    esink = sb("esink", [64, cfg.SWA_H])
    w2aug = sb("w2aug", [33, cfg.GK])
    glanw = sb("glanw", [128, 4])
    gdnnw = sb("gdnnw", [128, 1])
    cw = sb("cw", [128, cfg.CONVC // 128, 4])
    negA = sb("negA", [64, cfg.GDN_H])
    dtb = sb("dtb", [64, cfg.GDN_H])
    Sg = sb("Sg", [128, cfg.GLA_H * 2, 512])
    Sd = sb("Sd", [128, cfg.GDN_H, 128])
    convst = sb("convst", [128, cfg.CONVC // 128, NCH, 3])
    kT = sb("kT", [64, cfg.SWA_KV, 2 + NCH, CH], BF16)
    Vt = sb("Vt", [64, 2 + NCH, cfg.SKV], BF16)
    SKV = cfg.SKV
    nF = max(2 * NCH * SKV + 2 * SKV, 2048) + 1024 + 512
    UNF = sb("UNF", [128, nF])
    nB = max(NCH * cfg.SWA_KV * 2 * CH + NCH * 2 * SKV + 8 * T + 3 * 512, 4160 + max(cfg.GLA_H * 2 * 512, cfg.GDN_H * 128))
    UNB = sb("UNB", [128, nB], BF16)

    def vw(U, p, off, shape):
        n = int(np.prod(shape))
        a = U[0:p, off:off + n]
        if len(shape) == 2:
            return a.rearrange("p (a b) -> p a b", a=shape[0])
        if len(shape) == 3:
            return a.rearrange("p (a b c) -> p a b c", a=shape[0], b=shape[1])
        if len(shape) == 4:
            return a.rearrange("p (a b c d) -> p a b c d", a=shape[0], b=shape[1], c=shape[2])
        return a
    oA = nF - 1024
    tmpA = vw(UNF, 128, oA, [8, 64])
    tmpB = vw(UNF, 128, oA + 512, [8, 64])
    Kt32 = vw(UNF, 64, 0, [NCH, SKV])
    Vt32 = vw(UNF, 64, NCH * SKV, [NCH, SKV])
    cache32 = vw(UNF, 64, 2 * NCH * SKV, [2, SKV])
    kTc = vw(UNB, 64, 0, [NCH, cfg.SWA_KV, 2, CH])
    o_ = NCH * cfg.SWA_KV * 2 * CH
    Vc = vw(UNB, 64, o_, [NCH, 2, SKV])
    o_ += NCH * 2 * SKV
    Qg = vw(UNB, 64, o_, [8, T])
    o_ += 8 * T
    PT = vw(UNB, 64, o_, [3, 512])
    stg = vw(UNF, 128, 0, [2, 1024])
    qT = vw(UNF, 128, 0, [2, T])
    kTg = vw(UNF, 128, 512, [2, T])
    lrT = vw(UNF, 33, 1024, [T])
    spt = vw(UNF, 64, 1280, [256])
    Ep = vw(UNF, 128, 1536, [2, 64])
    En = vw(UNF, 128, 1664, [2, 64])
    Gg = vw(UNB, 128, 0, [4, T])
    vtok = vw(UNB, 64, 1024, [NCH, 512])
    qd = vw(UNB, 128, 3072, [2, 64])
    kd = vw(UNB, 128, 3200, [2, 64])
    attm = vw(UNB, 64, 3328, [64])
    kdt = vw(UNB, 64, 3392, [256])
    onb = vw(UNB, 64, 3648, [512])
    cvb = vw(UNF, 128, 0, [3, NCH, 3 + CH])
    qkv = vw(UNF, 128, 832, [3, T])
    gtm = vw(UNF, 64, 1600, [NCH, cfg.GDN_H])
    btm = vw(UNF, 64, 1600 + NCH * cfg.GDN_H, [NCH, cfg.GDN_H])
    assert 1600 + 2 * NCH * cfg.GDN_H <= 1728
    g64 = [vw(UNF, 64, 1728 + 64 * i, [64]) for i in range(8)]
    Erow = vw(UNF, 128, 2240, [64])
    tok = vw(UNF, 64, 2304, [2, 128])
    assert 2560 <= oA
    qkb = vw(UNB, 128, 0, [2, T])
    Gz = vw(UNB, 128, 512, [T])
    g64b = [vw(UNB, 64, 768 + 64 * i, [64]) for i in range(2)]
    kqg = vw(UNB, 128, 896, [2, 64])
    tokb = vw(UNB, 64, 1024, [2, 128])
    Sgb = vw(UNB, 128, 4160, [cfg.GLA_H * 2, 512])
    Sdb = vw(UNB, 128, 4160, [cfg.GDN_H, 128])
    st4 = sb("st4", [128, 4, T])
    sm = sb("sm", [128, 64])

    def act(func, out, in_, **kw):
        return P.op('act', 'activation', out=out, in_=in_, func=func, **kw)

    try:
      for l in range(L):
          for i in range(3):
              P.dma('sp', out=V(lnp[:, 2 * i, :], 'lnp'), in_=V(lng[i][l].rearrange("(c p) -> p c", p=128), 'dramc'),
                    allow_slow_non_contiguous=True)
              P.dma('sp', out=V(lnp[:, 2 * i + 1, :], 'lnp'), in_=V(lnb[i][l].rearrange("(c p) -> p c", p=128), 'dramc'),
                    allow_slow_non_contiguous=True)
          P.dma('sp', out=V(esink[:, :], 'esink'), in_=V(sinks[l].partition_broadcast(64), 'dramc'))
          act(AF.Exp, V(esink[:, :], 'esink'), V(esink[:, :], 'esink'))
          P.op('dve', 'memset', outs=('ap',), ap=V(w2aug[:, :], 'w2aug'), constant=0.0)
          P.dma('sp', out=V(w2aug[0:16, :], 'w2aug'), in_=V(w_gate2[l], 'dramc'))
          P.dma('sp', out=V(w2aug[32:33, :], 'w2aug'), in_=V(gate_bias[l:l + 1, :], 'dramc'))
          P.dma('sp', out=V(glanw[:, :], 'glanw'), in_=V(gla_nw[l].rearrange("(c p) -> p c", p=128), 'dramc'),
                allow_slow_non_contiguous=True)
          P.dma('sp', out=V(gdnnw[:, :], 'gdnnw'), in_=V(gdn_nw[l].rearrange("(c p) -> p c", p=128), 'dramc'),
                allow_slow_non_contiguous=True)
          for j in range(4):
              P.dma('sp', out=V(cw[:, :, j], 'cw'), in_=V(conv_w[l, j].rearrange("(c p) -> p c", p=128), 'dramc'),
                    allow_slow_non_contiguous=True)
          P.dma('sp', out=V(negA[:, :], 'negA'), in_=V(a_log[l].partition_broadcast(64), 'dramc'))
          act(AF.Exp, V(negA[:, :], 'negA'), V(negA[:, :], 'negA'))
          P.op('dve', 'tensor_scalar_mul', out=V(negA[:, :], 'negA'), in0=V(negA[:, :], 'negA'), scalar1=-1.0)
          P.dma('sp', out=V(dtb[:, :], 'dtb'), in_=V(dt_bias[l].partition_broadcast(64), 'dramc'))

          ckpt('params')
          for ti in range(cfg.NT):
              t0 = ti * T
              wctx['ti'], wctx['idx'] = ti, 0
              is_prompt = ti < cfg.PPC * cfg.PT
              if is_prompt:
                  pseq, ptile = divmod(ti, cfg.PT)
                  chunks = [dict(slot=pseq, c=ptile * NCH + i, first=(ptile == 0 and i == 0),
                                 last=(ptile == cfg.PT - 1 and i == NCH - 1), sample=None) for i in range(NCH)]
              else:
                  st_ = (ti - cfg.PPC * cfg.PT) * NCH
                  chunks = [dict(slot=cfg.PPC + st_ + i, c=0, first=True, last=True, sample=st_ + i) for i in range(NCH)]

              if l == 0:
                  for half in range(T // 128):
                      for q4 in range(D // 1024):
                          sv = V(stg[:, (half * 4 + q4) % 2, :], 'stg%d' % ((half * 4 + q4) % 2))
                          P.dma('sp', out=sv, in_=V(xin[t0 + half * 128:t0 + half * 128 + 128, q4 * 1024:(q4 + 1) * 1024], 'dramx'))
                          for c4 in range(2):
                              pt, pn = ps()
                              for c in range(4):
                                  P.op('pe', 'transpose', out=V(pt[:, c * 128:(c + 1) * 128], pn),
                                       in_=V(stg[:, (half * 4 + q4) % 2, (c4 * 4 + c) * 128:(c4 * 4 + c + 1) * 128], sv.res),
                                       identity=V(I32, RC))
                              cc = q4 * 8 + c4 * 4
                              P.op('dve', 'tensor_copy', out=V(X[:, cc:cc + 4, half * 128:(half + 1) * 128], 'X'),
                                   in_=V(pt[:, :].rearrange("p (c t) -> p c t", c=4), pn))
              else:
                  P.dma('sp', out=V(X[:, :, :], 'X'), in_=V(scratch[ti].rearrange("p (c t) -> p c t", c=KC), 'dram_scr%d' % ti))
              for c8 in range(0, KC, 8):
                  e = 'act' if (c8 // 8) % 2 else 'dve'
                  if e == 'act':
                      act(AF.Copy, V(xb[:, c8:c8 + 8, :], 'xb'), V(X[:, c8:c8 + 8, :], 'X'))
                  else:
                      P.op('dve', 'tensor_copy', out=V(xb[:, c8:c8 + 8, :], 'xb'), in_=V(X[:, c8:c8 + 8, :], 'X'))

              ckpt('load')
              def proj_fm(Wd, col0, ncols, M, fn, rhs=None, nk=KC, prow=128):
                  grp = max(M, min(ncols, (WS // nk) // M * M, 512))
                  for g0 in range(0, ncols, grp):
                      gn = min(grp, ncols - g0)
                      wt, wn = wload(Wd[:, col0 + g0:col0 + g0 + gn], nk, gn, prow)
                      for j in range(gn // M):
                          pt, pn = ps()
                          for k in range(nk):
                              r = rhs(k) if rhs else V(xb[:, k, :], 'xb')
                              P.op('pe', 'matmul', out=V(pt[0:M, 0:T], pn), lhsT=V(wt[:, k, j * M:(j + 1) * M], wn),
                                   rhs=r, start=(k == 0), stop=(k == nk - 1))
                          fn((g0 // M) + j, pt[0:M, 0:T], pn)

              def proj_tm(Wd, col0, ncols, fn):
                  gsz = min(ncols, WS // KC)
                  groups = []
                  for g0 in range(0, ncols, gsz):
                      gn = min(gsz, ncols - g0)
                      wt, wn = wload(Wd[:, col0 + g0:col0 + g0 + gn], KC, gn)
                      groups.append((g0, gn, wt, wn))
                  assert len(groups) <= 2
                  for ci in range(NCH):
                      pt, pn = ps()
                      for (g0, gn, wt, wn) in groups:
                          for k in range(KC):
                              P.op('pe', 'matmul', out=V(pt[0:64, g0:g0 + gn], pn), lhsT=V(xb[:, k, ci * CH:(ci + 1) * CH], 'xb'),
                                   rhs=V(wt[:, k, :], wn), start=(k == 0), stop=(k == KC - 1))
                      fn(ci, pt[0:64, 0:ncols], pn)
              Wl = w_in[l]

              P.barrier()
              for g in range(cfg.SWA_KV):
                  def k_fn(j, pa, pn, g=g):
                      P.op('dve', 'tensor_copy', out=V(kT[:, g, 2:2 + NCH, :], 'kT'), in_=V(pa.rearrange("p (c t) -> p c t", c=NCH), pn))
                  proj_fm(Wl, cfg.o_sk + g * 64, 64, 64, k_fn)
              def ktok_fn(ci, pa, pn):
                  P.op('dve', 'tensor_copy', out=V(Kt32[:, ci, :], 'Kt32'), in_=V(pa, pn))
              proj_tm(Wl, cfg.o_sk, cfg.SKV, ktok_fn)
              def vtok_fn(ci, pa, pn):
                  P.op('dve', 'tensor_copy', out=V(Vt32[:, ci, :], 'Vt32'), in_=V(pa, pn))
                  act(AF.Copy, V(Vt[:, 2 + ci, :], 'Vt'), V(Vt32[:, ci, :], 'Vt32'))
              proj_tm(Wl, cfg.o_sv, cfg.SKV, vtok_fn)

              ckpt('swa_kv')
              for ci, chk in enumerate(chunks):
                  blocks = []
                  if chk['sample'] is not None:
                      sidx = chk['sample']
                      for cin, dstT in ((ck_in, True), (cv_in, False)):
                          P.dma('sp', out=V(cache32[:, :, :], 'cache32'),
                                in_=V(cin[l, sidx].rearrange("(b t) f -> t b f", b=2), 'dramc'))
                          if not dstT:
                              P.op('dve', 'tensor_copy', out=V(Vc[:, ci, :, :], 'Vc'), in_=V(cache32[:, :, :], 'cache32'))
                          else:
                              for b in range(2):
                                  pt, pn = ps()
                                  for g in range(cfg.SWA_KV):
                                      P.op('pe', 'transpose', out=V(pt[0:64, g * 64:(g + 1) * 64], pn),
                                           in_=V(cache32[:, b, g * 64:(g + 1) * 64], 'cache32'), identity=V(I32[0:64, 0:64], RC))
                                  P.op('dve', 'tensor_copy', out=V(kTc[:, ci, :, b, :], 'kTc'),
                                       in_=V(pt[0:64, 0:cfg.SKV].rearrange("p (g t) -> p g t", g=cfg.SWA_KV), pn))
                          od = o_k if dstT else o_v
                          P.dma('sp', out=V(od[l, chk['slot'], 0:64, :], 'dram_o'), in_=V(cin[l, sidx, 64:128, :], 'dramc'))
                      blocks = [(0, lambda g, ci=ci: V(kTc[:, ci, g, 0, :], 'kTc'), lambda g, ci=ci: V(Vc[:, ci, 0, g * 64:(g + 1) * 64], 'Vc')),
                                (1, lambda g, ci=ci: V(kTc[:, ci, g, 1, :], 'kTc'), lambda g, ci=ci: V(Vc[:, ci, 1, g * 64:(g + 1) * 64], 'Vc'))]
                  else:
                      for j in range(2):
                          if chk['c'] - 2 + j >= 0:
                              sl = ci + j
                              blocks.append((j, (lambda g, sl=sl: V(kT[:, g, sl, :], 'kT')),
                                             (lambda g, sl=sl: V(Vt[:, sl, g * 64:(g + 1) * 64], 'Vt'))))
                  blocks.append((2, (lambda g, sl=ci + 2: V(kT[:, g, sl, :], 'kT')),
                                 (lambda g, sl=ci + 2: V(Vt[:, sl, g * 64:(g + 1) * 64], 'Vt'))))
                  chk['blocks'] = blocks
              ckpt('swa_cache')
              for g in range(cfg.SWA_KV):
                  def q_fn(j, pa, pn):
                      act(AF.Copy, V(Qg[:, j, :], 'Qg'), V(pa, pn), scale=0.125)
                  proj_fm(Wl, cfg.o_sq + g * cfg.GRP * 64, cfg.GRP * 64, 64, q_fn)
                  for ci, chk in enumerate(chunks):
                      nb = len(chk['blocks'])
                      for bi, (j, kf, vf) in enumerate(chk['blocks']):
                          pt, pn = ps()
                          for h in range(cfg.GRP):
                              P.op('pe', 'matmul', out=V(pt[0:64, h * 64:(h + 1) * 64], pn), lhsT=kf(g),
                                   rhs=V(Qg[:, h, ci * CH:(ci + 1) * CH], 'Qg'), start=True, stop=True)
                          ckpt('a1')
                          for h in range(cfg.GRP):
                              P.op('dve', 'tensor_scalar', out=V(tmpB[0:64, h, :], 'tmpB'), in0=V(Dj[j], RC),
                                   scalar1=-slopes[g * cfg.GRP + h], scalar2=None, op0=ALU.mult)
                              P.op('dve', 'tensor_tensor', out=V(tmpA[0:64, h, :], 'tmpA'), in0=V(tmpB[0:64, h, :], 'tmpB'),
                                   in1=V(pt[0:64, h * 64:(h + 1) * 64], pn), op=ALU.add)
                          ckpt('a2')
                          act(AF.Exp, V(PT[:, bi, :], 'PT%d' % bi), V(tmpA[0:64, :, :].rearrange("p h q -> p (h q)"), 'tmpA'))
                          ckpt('a3')
                      po, pon = ps()
                      pd, pdn = ps()
                      for bi, (j, kf, vf) in enumerate(chk['blocks']):
                          P.op('pe', 'matmul', out=V(po[0:64, 0:512], pon), lhsT=vf(g), rhs=V(PT[:, bi, :], 'PT%d' % bi),
                               start=(bi == 0), stop=(bi == nb - 1))
                      for bi, (j, kf, vf) in enumerate(chk['blocks']):
                          P.op('pe', 'matmul', out=V(pd[0:64, 0:512], pdn), lhsT=V(ONESbf[0:64, 0:64], RC),
                               rhs=V(PT[:, bi, :], 'PT%d' % bi), start=(bi == 0), stop=(bi == nb - 1))
                      for h in range(cfg.GRP):
                          hh = g * cfg.GRP + h
                          ckpt('a4')
                          P.op('dve', 'tensor_scalar', out=V(tmpB[0:64, h, :], 'tmpB'), in0=V(pd[0:64, h * 64:(h + 1) * 64], pdn),
                               scalar1=V(esink[:, hh:hh + 1], 'esink'), scalar2=None, op0=ALU.add)
                      P.op('dve', 'reciprocal', out=V(tmpB[0:64, :, :], 'tmpB'), in_=V(tmpB[0:64, :, :], 'tmpB'))
                      ckpt('a5')
                      P.op('dve', 'tensor_tensor', out=V(oTa[:, g * cfg.GRP:(g + 1) * cfg.GRP, ci * CH:(ci + 1) * CH], 'oTa'),
                           in0=V(po[0:64, 0:512].rearrange("p (h q) -> p h q", h=cfg.GRP), pon), in1=V(tmpB[0:64, :, :], 'tmpB'),
                           op=ALU.mult)
              ckpt('swa_attn')
              for ci, chk in enumerate(chunks):
                  if chk['sample'] is not None:
                      P.dma('sp', out=V(o_k[l, chk['slot'], 64:128, :], 'dram_o'), in_=V(Kt32[:, ci, :], 'Kt32'))
                      P.dma('sp', out=V(o_v[l, chk['slot'], 64:128, :], 'dram_o'), in_=V(Vt32[:, ci, :], 'Vt32'))
                  elif chk['last'] or chunks[-1]['last'] and ci == NCH - 2:
                      r0 = 0 if ci == NCH - 2 else 64
                      P.dma('sp', out=V(o_k[l, chk['slot'], r0:r0 + 64, :], 'dram_o'), in_=V(Kt32[:, ci, :], 'Kt32'))
                      P.dma('sp', out=V(o_v[l, chk['slot'], r0:r0 + 64, :], 'dram_o'), in_=V(Vt32[:, ci, :], 'Vt32'))
              if is_prompt:
                  P.op('dve', 'tensor_copy', out=V(tmpA[0:64, 0:cfg.SWA_KV, :], 'tmpA'), in_=V(kT[:, :, NCH, :], 'kT'))
                  P.op('dve', 'tensor_copy', out=V(kT[:, :, 0, :], 'kT'), in_=V(tmpA[0:64, 0:cfg.SWA_KV, :], 'tmpA'))
                  P.op('dve', 'tensor_copy', out=V(tmpA[0:64, 0:cfg.SWA_KV, :], 'tmpA'), in_=V(kT[:, :, NCH + 1, :], 'kT'))
                  P.op('dve', 'tensor_copy', out=V(kT[:, :, 1, :], 'kT'), in_=V(tmpA[0:64, 0:cfg.SWA_KV, :], 'tmpA'))
                  P.op('act', 'activation', out=V(PT[:, 0, 0:2 * cfg.SKV].rearrange("p (a f) -> p a f", a=2), 'PT0'),
                       in_=V(Vt[:, NCH:NCH + 2, :], 'Vt'), func=AF.Copy)
                  P.op('act', 'activation', out=V(Vt[:, 0:2, :], 'Vt'),
                       in_=V(PT[:, 0, 0:2 * cfg.SKV].rearrange("p (a f) -> p a f", a=2), 'PT0'), func=AF.Copy)

              ckpt('swa')
              P.barrier()
              P.op('dve', 'memset', outs=('ap',), ap=V(lrT[:, :], 'lrT'), constant=0.0)
              P.op('dve', 'memset', outs=('ap',), ap=V(lrT[32:33, :], 'lrT'), constant=1.0)
              def lr_fn(j, pa, pn):
                  P.op('dve', 'tensor_copy', out=V(lrT[0:16, :], 'lrT'), in_=V(pa, pn))
              proj_fm(Wl, cfg.o_glr, 16, 16, lr_fn)
              for h in range(cfg.GLA_H):
                  proj_fm(Wl, cfg.o_gq + h * 256, 256, 128,
                          lambda j, pa, pn: P.op('dve', 'tensor_copy', out=V(qT[:, j, :], 'qT'), in_=V(pa, pn)))
                  proj_fm(Wl, cfg.o_gk + h * 256, 256, 128,
                          lambda j, pa, pn: act(AF.Copy, V(kTg[:, j, :], 'kTg'), V(pa, pn)))

                  def r_fn(j, pa, pn):
                      act(AF.Silu, V(st4[:, 0, :], 'st4'), V(pa, pn))
                      P.op('dve', 'tensor_scalar', out=V(Gg[:, j, :], 'Gg'), in0=V(st4[:, 0, :], 'st4'),
                           scalar1=V(glanw[:, j:j + 1], 'glanw'), scalar2=None, op0=ALU.mult)
                  proj_fm(Wl, cfg.o_gr + h * 512, 512, 128, r_fn)
                  proj_tm(Wl, cfg.o_gv + h * 512, 512,
                          lambda ci, pa, pn: act(AF.Copy, V(vtok[:, ci, :], 'vtok'), V(pa, pn)))
                  for ci, chk in enumerate(chunks):
                      cs = slice(ci * CH, (ci + 1) * CH)
                      Sres = 'Sg%d' % h
                      if chk['first']:
                          if chk['sample'] is not None:
                              P.dma('sp', out=V(Sg[:, 2 * h:2 * h + 2, :], Sres),
                                    in_=V(gla_in[l, chk['sample'], h * 256:(h + 1) * 256, :].rearrange("(k p) v -> p k v", p=128), 'dramc'))
                          else:
                              P.op('dve', 'memset', outs=('ap',), ap=V(Sg[:, 2 * h:2 * h + 2, :], Sres), constant=0.0)
                          act(AF.Copy, V(Sgb[:, 2 * h:2 * h + 2, :], Sres + 'b'), V(Sg[:, 2 * h:2 * h + 2, :], Sres))
                      elif ci == 0:
                          act(AF.Copy, V(Sgb[:, 2 * h:2 * h + 2, :], Sres + 'b'), V(Sg[:, 2 * h:2 * h + 2, :], Sres))
                      pt, pn = ps()
                      P.op('pe', 'matmul', out=V(pt[0:64, 0:256], pn), lhsT=V(lrT[0:33, cs], 'lrT'),
                           rhs=V(w2aug[0:33, h * 256:(h + 1) * 256], 'w2aug'), start=True, stop=True)
                      act(AF.Exp, V(spt[:, :], 'spt'), V(pt[0:64, 0:256], pn), scale=-1.0)
                      act(AF.Ln, V(spt[:, :], 'spt'), V(spt[:, :], 'spt'), bias=1.0)
                      pc, pcn = ps()
                      for k in range(2):
                          P.op('pe', 'matmul', out=V(pc[:, k * 64:(k + 1) * 64], pcn), lhsT=V(spt[:, k * 128:(k + 1) * 128], 'spt'),
                               rhs=V(U32, RC), start=True, stop=True)
                      pcv = pc[:, 0:128].rearrange("p (k t) -> p k t", k=2)
                      act(AF.Exp, V(Ep[:, :, :], 'Ep'), V(pcv, pcn), scale=-1.0 / 16)
                      act(AF.Exp, V(En[:, :, :], 'En'), V(pcv, pcn), scale=1.0 / 16)
                      P.op('dve', 'scalar_tensor_tensor', out=V(qd[:, :, :], 'qd'), in0=V(qT[:, :, cs], 'qT'), scalar=256 ** -0.5,
                           in1=V(Ep[:, :, :], 'Ep'), op0=ALU.mult, op1=ALU.mult)
                      P.op('dve', 'tensor_tensor', out=V(kd[:, :, :], 'kd'), in0=V(kTg[:, :, cs], 'kTg'), in1=V(En[:, :, :], 'En'), op=ALU.mult)
                      pa_, pan = ps()
                      for k in range(2):
                          P.op('pe', 'matmul', out=V(pa_[0:64, 0:64], pan), lhsT=V(kd[:, k, :], 'kd'), rhs=V(qd[:, k, :], 'qd'),
                               start=(k == 0), stop=(k == 1))
                      P.op('dve', 'tensor_tensor', out=V(attm[:, :], 'attm'), in0=V(pa_[0:64, 0:64], pan), in1=V(U32, RC), op=ALU.mult)
                      pk, pkn = ps()
                      for k in range(2):
                          P.op('pe', 'matmul', out=V(pk[0:64, k * 128:(k + 1) * 128], pkn), lhsT=V(kd[:, k, :], 'kd'),
                               rhs=V(Ibf, RC), start=True, stop=True)
                      act(AF.Copy, V(kdt[:, :], 'kdt'), V(pk[0:64, 0:256], pkn))
                      po, pon = ps()
                      P.op('pe', 'matmul', out=V(po[0:64, 0:512], pon), lhsT=V(attm[:, :], 'attm'), rhs=V(vtok[:, ci, :], 'vtok'),
                           start=True, stop=False)
                      for k in range(2):
                          P.op('pe', 'matmul', out=V(po[0:64, 0:512], pon), lhsT=V(qd[:, k, :], 'qd'), rhs=V(Sgb[:, 2 * h + k, :], Sres + 'b'),
                               start=False, stop=(k == 1))
                      for k in range(2):
                          pS, pSn = ps()
                          P.op('pe', 'matmul', out=V(pS[:, 0:512], pSn), lhsT=V(kdt[:, k * 128:(k + 1) * 128], 'kdt'),
                               rhs=V(vtok[:, ci, :], 'vtok'), start=True, stop=True)
                          P.op('dve', 'tensor_tensor', out=V(Sg[:, 2 * h + k, :], Sres), in0=V(Sg[:, 2 * h + k, :], Sres),
                               in1=V(pS[:, 0:512], pSn), op=ALU.add)
                          P.op('dve', 'tensor_scalar', out=V(Sg[:, 2 * h + k, :], Sres), in0=V(Sg[:, 2 * h + k, :], Sres),
                               scalar1=V(Ep[:, k, 63:64], 'Ep'), scalar2=None, op0=ALU.mult)
                      act(AF.Copy, V(Sgb[:, 2 * h:2 * h + 2, :], Sres + 'b'), V(Sg[:, 2 * h:2 * h + 2, :], Sres))
                      act(AF.Square, V(tmpA[0:64, :, :].rearrange("p a b -> p (a b)"), 'tmpA'), V(po[0:64, 0:512], pon),
                          accum_out=V(sm[0:64, 0:1], 'sm'))
                      P.op('dve', 'tensor_scalar', out=V(sm[0:64, 1:2], 'sm'), in0=V(sm[0:64, 0:1], 'sm'), scalar1=1.0 / 512, scalar2=1e-6,
                           op0=ALU.mult, op1=ALU.add)
                      act(AF.Sqrt, V(sm[0:64, 2:3], 'sm'), V(sm[0:64, 1:2], 'sm'))
                      P.op('dve', 'reciprocal', out=V(sm[0:64, 2:3], 'sm'), in_=V(sm[0:64, 2:3], 'sm'))
                      P.op('dve', 'tensor_scalar', out=V(onb[:, :], 'onb'), in0=V(po[0:64, 0:512], pon), scalar1=V(sm[0:64, 2:3], 'sm'),
                           scalar2=None, op0=ALU.mult)
                      ptr, ptn = ps()
                      for vc in range(4):
                          P.op('pe', 'matmul', out=V(ptr[:, vc * 64:(vc + 1) * 64], ptn), lhsT=V(onb[:, vc * 128:(vc + 1) * 128], 'onb'),
                               rhs=V(Ibf[0:64, 0:64], RC), start=True, stop=True)
                      P.op('dve', 'tensor_tensor', out=V(oTb[:, 4 * h:4 * h + 4, cs], 'oTb'),
                           in0=V(ptr[:, 0:256].rearrange("p (v t) -> p v t", v=4), ptn), in1=V(Gg[:, :, cs], 'Gg'), op=ALU.mult)
                      if chk['last']:
                          P.dma('sp', out=V(o_gla[l, chk['slot'], h * 256:(h + 1) * 256, :].rearrange("(k p) v -> p k v", p=128), 'dram_o'),
                                in_=V(Sg[:, 2 * h:2 * h + 2, :], Sres))

              ckpt('gla')
              P.barrier()
              def gb_fn(ci, pa, pn):
                  H = cfg.GDN_H
                  act(AF.Sigmoid, V(btm[:, ci, :], 'btm'), V(pa[:, 0:H], pn))
                  P.op('dve', 'tensor_tensor', out=V(gtm[:, ci, :], 'gtm'), in0=V(pa[:, H:2 * H], pn), in1=V(dtb[:, :], 'dtb'), op=ALU.add)
                  act(AF.Exp, V(gtm[:, ci, :], 'gtm'), V(gtm[:, ci, :], 'gtm'))
                  act(AF.Ln, V(gtm[:, ci, :], 'gtm'), V(gtm[:, ci, :], 'gtm'), bias=1.0)
                  P.op('dve', 'tensor_tensor', out=V(gtm[:, ci, :], 'gtm'), in0=V(gtm[:, ci, :], 'gtm'), in1=V(negA[:, :], 'negA'), op=ALU.mult)
              proj_tm(Wl, cfg.o_db, 2 * cfg.GDN_H, gb_fn)
              for h in range(cfg.GDN_H):
                  Sres = 'Sd%d' % h
                  for a in range(3):
                      col = cfg.o_dqkv + a * cfg.DK + h * 128

                      def c_fn(j, pa, pn, a=a):
                          P.op('dve', 'tensor_copy', out=V(cvb[:, a, :, 3:3 + CH], 'cvb'), in_=V(pa.rearrange("p (c t) -> p c t", c=NCH), pn))
                      proj_fm(Wl, col, 128, 128, c_fn)

                  def z_fn(j, pa, pn):
                      act(AF.Silu, V(st4[:, 0, :], 'st4'), V(pa, pn))
                      P.op('dve', 'tensor_scalar', out=V(Gz[:, :], 'Gz'), in0=V(st4[:, 0, :], 'st4'),
                           scalar1=V(gdnnw[:, 0:1], 'gdnnw'), scalar2=None, op0=ALU.mult)
                  proj_fm(Wl, cfg.o_dz + h * 128, 128, 128, z_fn)
                  for ci, chk in enumerate(chunks):
                      for a in range(3):
                          cidx = a * cfg.GDN_H + h
                          if chk['first']:
                              if chk['sample'] is not None:
                                  P.dma('sp', out=V(cvb[:, a, ci, 0:3], 'cvb'),
                                        in_=V(conv_in[l, chk['sample'], :, cidx * 128:(cidx + 1) * 128].rearrange("t p -> p t"), 'dramc'),
                                        allow_slow_non_contiguous=True)
                              else:
                                  P.op('dve', 'memset', outs=('ap',), ap=V(cvb[:, a, ci, 0:3], 'cvb'), constant=0.0)
                          elif ci == 0:
                              P.op('dve', 'tensor_copy', out=V(cvb[:, a, ci, 0:3], 'cvb'), in_=V(convst[:, cidx, NCH - 1, :], 'convst'))
                          else:
                              P.op('dve', 'tensor_copy', out=V(cvb[:, a, ci, 0:3], 'cvb'), in_=V(cvb[:, a, ci - 1, CH:CH + 3], 'cvb'))
                      if ci == NCH - 1 or chk['last']:
                          for a in range(3):
                              cidx = a * cfg.GDN_H + h
                              P.op('dve', 'tensor_copy', out=V(convst[:, cidx, ci, :], 'convst'), in_=V(cvb[:, a, ci, CH:CH + 3], 'cvb'))
                      if chk['last'] and h == cfg.GDN_H - 1:
                          for tt in range(3):
                              P.dma('sp', out=V(o_conv[l, chk['slot'], tt].rearrange("(c p) -> p c", p=128), 'dram_o'),
                                    in_=V(convst[:, :, ci, tt], 'convst'), allow_slow_non_contiguous=True)
                  for a in range(3):
                      cidx = a * cfg.GDN_H + h
                      for ci in range(NCH):
                          o_ = V(qkv[:, a, ci * CH:(ci + 1) * CH], 'qkv')
                          P.op('dve', 'tensor_scalar', out=o_, in0=V(cvb[:, a, ci, 0:CH], 'cvb'), scalar1=V(cw[:, cidx, 0:1], 'cw'),
                               scalar2=None, op0=ALU.mult)
                          for j in range(1, 4):
                              P.op('dve', 'scalar_tensor_tensor', out=o_, in0=V(cvb[:, a, ci, j:j + CH], 'cvb'),
                                   scalar=V(cw[:, cidx, j:j + 1], 'cw'), in1=o_, op0=ALU.mult, op1=ALU.add)
                      act(AF.Silu, V(qkv[:, a, :], 'qkv'), V(qkv[:, a, :], 'qkv'))
                  for a in range(2):
                      act(AF.Square, V(st4[:, 1, :], 'st4'), V(qkv[:, a, :], 'qkv'))
                      pt, pn = ps()
                      P.op('pe', 'matmul', out=V(pt[:, 0:T], pn), lhsT=V(ONES32, RC), rhs=V(st4[:, 1, :], 'st4'), start=True, stop=True)
                      P.op('dve', 'tensor_scalar', out=V(st4[:, 2, :], 'st4'), in0=V(pt[:, 0:T], pn), scalar1=1e-6, scalar2=None, op0=ALU.add)
                      act(AF.Sqrt, V(st4[:, 2, :], 'st4'), V(st4[:, 2, :], 'st4'))
                      P.op('dve', 'reciprocal', out=V(st4[:, 2, :], 'st4'), in_=V(st4[:, 2, :], 'st4'))
                      P.op('dve', 'scalar_tensor_tensor', out=V(qkv[:, a, :], 'qkv'), in0=V(qkv[:, a, :], 'qkv'),
                           scalar=(128 ** -0.5 if a == 0 else 1.0), in1=V(st4[:, 2, :], 'st4'), op0=ALU.mult, op1=ALU.mult)
                      act(AF.Copy, V(qkb[:, a, :], 'qkb'), V(qkv[:, a, :], 'qkv'))
                  for ci, chk in enumerate(chunks):
                      cs = slice(ci * CH, (ci + 1) * CH)
                      if chk['first']:
                          if chk['sample'] is not None:
                              P.dma('sp', out=V(Sd[:, h, :], Sres), in_=V(gdn_in[l, chk['sample'], h * 128:(h + 1) * 128, :], 'dramc'))
                          else:
                              P.op('dve', 'memset', outs=('ap',), ap=V(Sd[:, h, :], Sres), constant=0.0)
                          act(AF.Copy, V(Sdb[:, h, :], Sres + 'b'), V(Sd[:, h, :], Sres))
                      elif ci == 0:
                          act(AF.Copy, V(Sdb[:, h, :], Sres + 'b'), V(Sd[:, h, :], Sres))
                      gcol = V(gtm[:, ci, h:h + 1], 'gtm')
                      bcol = V(btm[:, ci, h:h + 1], 'btm')
                      Ug, decT, decTm, decUs, Nm, NT, Xm, Mt = [(g64[i], 'g64_%d' % i) for i in range(8)]
                      P.op('dve', 'tensor_scalar', out=V(Ug[0][:, :], Ug[1]), in0=V(U32, RC), scalar1=gcol, scalar2=None, op0=ALU.mult)
                      pg, pgn = ps()
                      P.op('pe', 'matmul', out=V(pg[:, 0:64], pgn), lhsT=V(ONES32[0:64, :], RC), rhs=V(Ug[0][:, :], Ug[1]), start=True, stop=True)
                      P.op('pe', 'matmul', out=V(pg[0:64, 64:65], pgn), lhsT=V(U32, RC), rhs=gcol, start=True, stop=True)
                      P.op('dve', 'tensor_scalar', out=V(sm[0:64, 4:5], 'sm'), in0=V(pg[0:64, 64:65], pgn), scalar1=-1.0, scalar2=None, op0=ALU.mult)
                      act(AF.Exp, V(Erow[:, :], 'Erow'), V(pg[:, 0:64], pgn))
                      P.op('dve', 'tensor_scalar', out=V(decT[0][:, :], decT[1]), in0=V(pg[0:64, 0:64], pgn), scalar1=V(sm[0:64, 4:5], 'sm'),
                           scalar2=0.0, op0=ALU.add, op1=ALU.min)
                      act(AF.Exp, V(decT[0][:, :], decT[1]), V(decT[0][:, :], decT[1]))
                      P.op('dve', 'tensor_tensor', out=V(decTm[0][:, :], decTm[1]), in0=V(decT[0][:, :], decT[1]), in1=V(U32, RC), op=ALU.mult)
                      P.op('dve', 'tensor_tensor', out=V(decUs[0][:, :], decUs[1]), in0=V(decT[0][:, :], decT[1]), in1=V(Us32, RC), op=ALU.mult)
                      for a in range(2):
                          P.op('dve', 'tensor_tensor', out=V(kqg[:, a, :], 'kqg'), in0=V(qkv[:, a, cs], 'qkv'), in1=V(Erow[:, :], 'Erow'), op=ALU.mult)
                      pkk, pkkn = ps()
                      P.op('pe', 'matmul', out=V(pkk[0:64, 0:64], pkkn), lhsT=V(qkb[:, 1, cs], 'qkb'), rhs=V(qkb[:, 1, cs], 'qkb'), start=True, stop=True)
                      P.op('pe', 'matmul', out=V(pkk[0:64, 64:128], pkkn), lhsT=V(qkb[:, 1, cs], 'qkb'), rhs=V(qkb[:, 0, cs], 'qkb'), start=True, stop=True)
                      P.op('dve', 'scalar_tensor_tensor', out=V(Nm[0][:, :], Nm[1]), in0=V(pkk[0:64, 0:64], pkkn), scalar=bcol,
                           in1=V(decUs[0][:, :], decUs[1]), op0=ALU.mult, op1=ALU.mult)
                      P.op('dve', 'tensor_tensor', out=V(g64b[0][:, :], 'g64b_0'), in0=V(pkk[0:64, 64:128], pkkn), in1=V(decTm[0][:, :], decTm[1]), op=ALU.mult)
                      ptn_, ptnn = ps()
                      P.op('pe', 'transpose', out=V(ptn_[0:64, 0:64], ptnn), in_=V(Nm[0][:, :], Nm[1]), identity=V(I32[0:64, 0:64], RC))
                      P.op('dve', 'tensor_copy', out=V(NT[0][:, :], NT[1]), in_=V(ptn_[0:64, 0:64], ptnn))
                      P.op('dve', 'tensor_tensor', out=V(Xm[0][:, :], Xm[1]), in0=V(I32[0:64, 0:64], RC), in1=V(Nm[0][:, :], Nm[1]), op=ALU.subtract)
                      Mc, MTc = Nm, NT
                      Mn, MTn = Ug, Mt
                      for step in range(5):
                          p2, p2n = ps()
                          P.op('pe', 'matmul', out=V(p2[0:64, 0:64], p2n), lhsT=V(MTc[0][:, :], MTc[1]), rhs=V(Mc[0][:, :], Mc[1]), start=True, stop=True)
                          P.op('pe', 'matmul', out=V(p2[0:64, 64:128], p2n), lhsT=V(Mc[0][:, :], Mc[1]), rhs=V(MTc[0][:, :], MTc[1]), start=True, stop=True)
                          P.op('dve', 'tensor_copy', out=V(Mn[0][:, :], Mn[1]), in_=V(p2[0:64, 0:64], p2n))
                          act(AF.Copy, V(MTn[0][:, :], MTn[1]), V(p2[0:64, 64:128], p2n))
                          p3, p3n = ps()
                          P.op('pe', 'matmul', out=V(p3[0:64, 0:64], p3n), lhsT=V(MTn[0][:, :], MTn[1]), rhs=V(Xm[0][:, :], Xm[1]), start=True, stop=True)
                          P.op('dve', 'tensor_tensor', out=V(Xm[0][:, :], Xm[1]), in0=V(Xm[0][:, :], Xm[1]), in1=V(p3[0:64, 0:64], p3n), op=ALU.add)
                          Mc, MTc, Mn, MTn = Mn, MTn, Mc, MTc
                      ptk, ptkn = ps()
                      P.op('pe', 'transpose', out=V(ptk[0:64, 0:128], ptkn), in_=V(qkv[:, 1, cs], 'qkv'), identity=V(I32, RC))
                      P.op('pe', 'transpose', out=V(ptk[0:64, 128:256], ptkn), in_=V(qkv[:, 2, cs], 'qkv'), identity=V(I32, RC))
                      P.op('dve', 'tensor_scalar', out=V(tokb[:, 0, :], 'tokb'), in0=V(ptk[0:64, 0:128], ptkn), scalar1=V(decT[0][:, 63:64], decT[1]),
                           scalar2=None, op0=ALU.mult)
                      pks, pksn = ps()
                      P.op('pe', 'matmul', out=V(pks[0:64, 0:128], pksn), lhsT=V(kqg[:, 1, :], 'kqg'), rhs=V(Sdb[:, h, :], Sres + 'b'), start=True, stop=True)
                      P.op('dve', 'tensor_copy', out=V(tok[:, 1, :], 'tok'), in_=V(ptk[0:64, 128:256], ptkn))
                      P.op('dve', 'tensor_tensor', out=V(tok[:, 0, :], 'tok'), in0=V(tok[:, 1, :], 'tok'), in1=V(pks[0:64, 0:128], pksn), op=ALU.subtract)
                      pvn, pvnn = ps()
                      P.op('pe', 'matmul', out=V(pvn[0:64, 0:128], pvnn), lhsT=V(Xm[0][:, :], Xm[1]), rhs=V(tok[:, 0, :], 'tok'), start=True, stop=True)
                      P.op('dve', 'tensor_scalar', out=V(tokb[:, 1, :], 'tokb'), in0=V(pvn[0:64, 0:128], pvnn), scalar1=bcol, scalar2=None, op0=ALU.mult)
                      po, pon = ps()
                      P.op('pe', 'matmul', out=V(po[0:64, 0:128], pon), lhsT=V(kqg[:, 0, :], 'kqg'), rhs=V(Sdb[:, h, :], Sres + 'b'), start=True, stop=False)
                      P.op('pe', 'matmul', out=V(po[0:64, 0:128], pon), lhsT=V(g64b[0][:, :], 'g64b_0'), rhs=V(tokb[:, 1, :], 'tokb'), start=False, stop=True)
                      pS, pSn = ps()
                      P.op('pe', 'matmul', out=V(pS[:, 0:128], pSn), lhsT=V(tokb[:, 0, :], 'tokb'), rhs=V(tokb[:, 1, :], 'tokb'), start=True, stop=True)
                      P.op('dve', 'scalar_tensor_tensor', out=V(Sd[:, h, :], Sres), in0=V(Sd[:, h, :], Sres), scalar=V(Erow[:, 63:64], 'Erow'),
                           in1=V(pS[:, 0:128], pSn), op0=ALU.mult, op1=ALU.add)
                      act(AF.Copy, V(Sdb[:, h, :], Sres + 'b'), V(Sd[:, h, :], Sres))
                      act(AF.Square, V(tmpA[0:64, 0:2, :].rearrange("p a b -> p (a b)"), 'tmpA'), V(po[0:64, 0:128], pon),
                          accum_out=V(sm[0:64, 8:9], 'sm'))
                      P.op('dve', 'tensor_scalar', out=V(sm[0:64, 9:10], 'sm'), in0=V(sm[0:64, 8:9], 'sm'), scalar1=1.0 / 128, scalar2=1e-6,
                           op0=ALU.mult, op1=ALU.add)
                      act(AF.Sqrt, V(sm[0:64, 10:11], 'sm'), V(sm[0:64, 9:10], 'sm'))
                      P.op('dve', 'reciprocal', out=V(sm[0:64, 10:11], 'sm'), in_=V(sm[0:64, 10:11], 'sm'))
                      P.op('dve', 'tensor_scalar', out=V(onb[:, 0:128], 'onb'), in0=V(po[0:64, 0:128], pon), scalar1=V(sm[0:64, 10:11], 'sm'),
                           scalar2=None, op0=ALU.mult)
                      ptr, ptn2 = ps()
                      P.op('pe', 'matmul', out=V(ptr[:, 0:64], ptn2), lhsT=V(onb[:, 0:128], 'onb'), rhs=V(Ibf[0:64, 0:64], RC), start=True, stop=True)
                      P.op('dve', 'tensor_tensor', out=V(oTc[:, h, cs], 'oTc'), in0=V(ptr[:, 0:64], ptn2), in1=V(Gz[:, cs], 'Gz'), op=ALU.mult)
                      if chk['last']:
                          P.dma('sp', out=V(o_gdn[l, chk['slot'], h * 128:(h + 1) * 128, :], 'dram_o'), in_=V(Sd[:, h, :], Sres))

              ckpt('gdn')
              P.barrier()
              brs = [(oTa, cfg.SQ // 64, 64, 'oTa'), (oTb, cfg.GV // 128, 128, 'oTb'), (oTc, cfg.DV // 128, 128, 'oTc')]
              CG = 256
              for c0 in range(0, D, CG):
                  for b in range(3):
                      ob, nk, prow, on = brs[b]
                      wg, wgn = wload(Wl[:, cfg.o_mg + b * D + c0:cfg.o_mg + b * D + c0 + CG], KC, CG)
                      for j in range(CG // 128):
                          pt, pn = ps()
                          for k in range(KC):
                              P.op('pe', 'matmul', out=V(pt[:, 0:T], pn), lhsT=V(wg[:, k, j * 128:(j + 1) * 128], wgn), rhs=V(xb[:, k, :], 'xb'),
                                   start=(k == 0), stop=(k == KC - 1))
                          act(AF.Sigmoid, V(st4[:, j, :], 'st4_%d' % j), V(pt[:, 0:T], pn))
                      wb, wbn = wload(w_br[b][l][:, c0:c0 + CG], nk, CG, prow)
                      for j in range(CG // 128):
                          pt, pn = ps()
                          for k in range(nk):
                              P.op('pe', 'matmul', out=V(pt[:, 0:T], pn), lhsT=V(wb[:, k, j * 128:(j + 1) * 128], wbn), rhs=V(ob[0:prow, k, :], on),
                                   start=(k == 0), stop=(k == nk - 1))
                          cj = c0 // 128 + j
                          if b == 0:
                              P.op('dve', 'tensor_tensor', out=V(st4[:, 2 + j, :], 'st4_%d' % (2 + j)), in0=V(pt[:, 0:T], pn), in1=V(st4[:, j, :], 'st4_%d' % j), op=ALU.mult)
                          else:
                              P.op('dve', 'tensor_tensor', out=V(st4[:, j, :], 'st4_%d' % j), in0=V(pt[:, 0:T], pn), in1=V(st4[:, j, :], 'st4_%d' % j), op=ALU.mult)
                              o_ = V(mb[:, cj, :], 'mb') if b == 2 else V(st4[:, 2 + j, :], 'st4_%d' % (2 + j))
                              P.op('dve', 'tensor_tensor', out=o_, in0=V(st4[:, 2 + j, :], 'st4_%d' % (2 + j)), in1=V(st4[:, j, :], 'st4_%d' % j), op=ALU.add)

              ckpt('merge')
              def accum_linear(Wd, nk, rhs_fn, first, prow=128):
                  for c0 in range(0, D, 256):
                      wt, wn = wload(Wd[:, c0:c0 + 256], nk, 256, prow)
                      for j in range(2):
                          pt, pn = ps()
                          for k in range(nk):
                              P.op('pe', 'matmul', out=V(pt[:, 0:T], pn), lhsT=V(wt[:, k, j * 128:(j + 1) * 128], wn), rhs=rhs_fn(k),
                                   start=(k == 0), stop=(k == nk - 1))
                          cj = c0 // 128 + j
                          xr = V(X[:, cj, :], 'X')
                          if first:
                              P.op('dve', 'scalar_tensor_tensor', out=xr, in0=xr, scalar=cfg.ALPHA, in1=V(pt[:, 0:T], pn), op0=ALU.mult, op1=ALU.add)
                          else:
                              P.op('dve', 'tensor_tensor', out=xr, in0=xr, in1=V(pt[:, 0:T], pn), op=ALU.add)

              def layer_norm(i):
                  p1, p1n = ps()
                  p2, p2n = ps()
                  for k in range(KC):
                      P.op('pe', 'matmul', out=V(p1[:, 0:T], p1n), lhsT=V(ONES32, RC), rhs=V(X[:, k, :], 'X'), start=(k == 0), stop=(k == KC - 1))
                  for k in range(KC):
                      sq = V(st4[:, k % 2, :], 'st4_%d' % (k % 2))
                      act(AF.Square, sq, V(X[:, k, :], 'X'))
                      P.op('pe', 'matmul', out=V(p2[:, 0:T], p2n), lhsT=V(ONES32, RC), rhs=sq, start=(k == 0), stop=(k == KC - 1))
                  mean, rstd = V(st4[:, 2, :], 'st4_2'), V(st4[:, 3, :], 'st4_3')
                  P.op('dve', 'tensor_scalar', out=mean, in0=V(p1[:, 0:T], p1n), scalar1=1.0 / D, scalar2=None, op0=ALU.mult)
                  P.op('dve', 'tensor_tensor', out=V(st4[:, 0, :], 'st4_0'), in0=mean, in1=mean, op=ALU.mult)
                  P.op('dve', 'scalar_tensor_tensor', out=rstd, in0=V(p2[:, 0:T], p2n), scalar=1.0 / D, in1=V(st4[:, 0, :], 'st4_0'),
                       op0=ALU.mult, op1=ALU.subtract)
                  P.op('dve', 'tensor_scalar', out=rstd, in0=rstd, scalar1=1e-5, scalar2=None, op0=ALU.add)
                  act(AF.Sqrt, rstd, rstd)
                  P.op('dve', 'reciprocal', out=rstd, in_=rstd)
                  for k in range(KC):
                      xr = V(X[:, k, :], 'X')
                      P.op('dve', 'tensor_tensor', out=xr, in0=xr, in1=mean, op=ALU.subtract)
                      P.op('dve', 'tensor_tensor', out=xr, in0=xr, in1=rstd, op=ALU.mult)
                      act(AF.Identity, xr, xr, scale=V(lnp[:, 2 * i, k:k + 1], 'lnp'), bias=V(lnp[:, 2 * i + 1, k:k + 1], 'lnp'))
                      P.op('dve', 'tensor_copy', out=V(xb[:, k, :], 'xb'), in_=xr)

              accum_linear(w_out[l], KC, lambda k: V(mb[:, k, :], 'mb'), True)
              layer_norm(0)
              HG = min(16, KC) * 128
              for h0 in range(0, cfg.DFF, HG):
                  hn = min(HG, cfg.DFF - h0)
                  for g0 in range(0, hn, 256):
                      wt, wn = wload(w_up[l][:, h0 + g0:h0 + g0 + 256], KC, 256)
                      for j in range(2):
                          pt, pn = ps()
                          for k in range(KC):
                              P.op('pe', 'matmul', out=V(pt[:, 0:T], pn), lhsT=V(wt[:, k, j * 128:(j + 1) * 128], wn), rhs=V(xb[:, k, :], 'xb'),
                                   start=(k == 0), stop=(k == KC - 1))
                          hj = g0 // 128 + j
                          act(AF.Relu, V(st4[:, 0, :], 'st4_0'), V(pt[:, 0:T], pn))
                          P.op('dve', 'tensor_tensor', out=V(mb[:, hj, :], 'mb'), in0=V(st4[:, 0, :], 'st4_0'), in1=V(st4[:, 0, :], 'st4_0'), op=ALU.mult)
                  accum_linear(w_down[l][h0:h0 + hn, :], hn // 128, lambda k: V(mb[:, k, :], 'mb'), h0 == 0)
              layer_norm(1)
              pT = mb
              for half in range(T // 128):
                  P.dma('sp', out=V(stg[:, 0, 0:cfg.PE_DIM], 'stg0'), in_=V(pin[l, t0 + half * 128:t0 + half * 128 + 128, :], 'dramx'))
                  pt, pn = ps()
                  for c in range(cfg.PE_DIM // 128):
                      P.op('pe', 'transpose', out=V(pt[:, c * 128:(c + 1) * 128], pn), in_=V(stg[:, 0, c * 128:(c + 1) * 128], 'stg0'), identity=V(I32, RC))
                  P.op('dve', 'tensor_copy', out=V(pT[:, 0:cfg.PE_DIM // 128, half * 128:(half + 1) * 128], 'mb'),
                       in_=V(pt[:, 0:cfg.PE_DIM].rearrange("p (c t) -> p c t", c=cfg.PE_DIM // 128), pn))
              for c0 in range(0, D, 256):
                  wg, wgn = wload(pe_g[l][:, c0:c0 + 256], KC, 256)
                  wp, wpn = wload(pe_p[l][:, c0:c0 + 256], cfg.PE_DIM // 128, 256)
                  for j in range(2):
                      pt, pn = ps()
                      for k in range(KC):
                          P.op('pe', 'matmul', out=V(pt[:, 0:T], pn), lhsT=V(wg[:, k, j * 128:(j + 1) * 128], wgn), rhs=V(xb[:, k, :], 'xb'),
                               start=(k == 0), stop=(k == KC - 1))
                      act(AF.Sigmoid, V(st4[:, j, :], 'st4_%d' % j), V(pt[:, 0:T], pn))
                      p2, p2n = ps()
                      nkp = cfg.PE_DIM // 128
                      for k in range(nkp):
                          P.op('pe', 'matmul', out=V(p2[:, 0:T], p2n), lhsT=V(wp[:, k, j * 128:(j + 1) * 128], wpn), rhs=V(pT[:, k, :], 'mb'),
                               start=(k == 0), stop=(k == nkp - 1))
                      P.op('dve', 'tensor_tensor', out=V(st4[:, j, :], 'st4_%d' % j), in0=V(st4[:, j, :], 'st4_%d' % j), in1=V(p2[:, 0:T], p2n), op=ALU.mult)
                      xr = V(X[:, c0 // 128 + j, :], 'X')
                      P.op('dve', 'scalar_tensor_tensor', out=xr, in0=xr, scalar=cfg.ALPHA, in1=V(st4[:, j, :], 'st4_%d' % j), op0=ALU.mult, op1=ALU.add)
              layer_norm(2)
              P.barrier()
              if l < L - 1:
                  P.dma('sp', out=V(scratch[ti].rearrange("p (c t) -> p c t", c=KC), 'dram_scr%d' % ti), in_=V(X[:, :, :], 'X'))
              else:
                  for half in range(T // 128):
                      for q4 in range(D // 1024):
                          si = (half * 4 + q4) % 2
                          for c4 in range(2):
                              pt, pn = ps()
                              for c in range(4):
                                  cc = q4 * 8 + c4 * 4 + c
                                  P.op('pe', 'transpose', out=V(pt[:, c * 128:(c + 1) * 128], pn), in_=V(X[:, cc, half * 128:(half + 1) * 128], 'X'),
                                       identity=V(I32, RC))
                              P.op('dve', 'tensor_copy', out=V(stg[:, si, c4 * 512:(c4 + 1) * 512], 'stg%d' % si), in_=V(pt[:, :], pn))
                          P.dma('sp', out=V(y[t0 + half * 128:t0 + half * 128 + 128, q4 * 1024:(q4 + 1) * 1024], 'dram_o'), in_=V(stg[:, si, :], 'stg%d' % si))
    except StopBuild:
        pass
    P.finish('sp')
    es.close()
    return nc, P.nins


FULL = Cfg()
_cache = {}


def run(cfg, inp):
    if id(cfg) not in _cache:
        _cache[id(cfg)] = build(cfg)
    nc, nins = _cache[id(cfg)]
    L = cfg.DEPTH
    f = lambda a: np.ascontiguousarray(np.asarray(a, dtype=np.float32))
    consts = make_consts()
    in_maps = []
    for c in range(cfg.NCORES):
        ps_ = slice(c * cfg.PPC, (c + 1) * cfg.PPC)
        ss = slice(c * cfg.SPC, (c + 1) * cfg.SPC)
        xin = np.concatenate([f(inp['x_prompt'])[ps_].reshape(-1, cfg.D), f(inp['x_sample'])[ss].reshape(-1, cfg.D)], 0)
        pin = np.concatenate([f(inp['p_prompt'])[:, ps_].reshape(L, -1, cfg.PE_DIM), f(inp['p_sample'])[:, ss].reshape(L, -1, cfg.PE_DIM)], 1)
        m = dict(xin=xin, pin=pin, consts=consts,
                 ck_in=f(inp['cache_swa_k'])[:, ss].reshape(L, cfg.SPC, 128, cfg.SKV),
                 cv_in=f(inp['cache_swa_v'])[:, ss].reshape(L, cfg.SPC, 128, cfg.SKV),
                 gla_in=f(inp['state_gla'])[:, ss].reshape(L, cfg.SPC, cfg.GK, 512),
                 gdn_in=f(inp['state_gdn'])[:, ss].reshape(L, cfg.SPC, cfg.DK, 128),
                 conv_in=f(inp['state_gdn_conv'])[:, ss])
        for k in ('w_in', 'swa_sinks', 'gla_w_gate2', 'gla_gate_bias', 'gla_norm_w', 'gdn_conv_w', 'gdn_a_log', 'gdn_dt_bias',
                  'gdn_norm_w', 'w_br_swa', 'w_br_gla', 'w_br_gdn', 'w_out', 'ln1_g', 'ln1_b', 'w_up', 'w_down', 'ln2_g', 'ln2_b',
                  'pe_w_gate', 'pe_w_proj', 'ln3_g', 'ln3_b'):
            m[k] = f(inp[k])
        in_maps.append({k: np.ascontiguousarray(v) for k, v in m.items()})
    res = run_bass_kernel_spmd(nc, in_maps, core_ids=list(range(cfg.NCORES))).results
    npt = cfg.PPC * cfg.SEQ
    yp = np.concatenate([r['y'][:npt].reshape(cfg.PPC, cfg.SEQ, cfg.D) for r in res], 0)
    ys = np.concatenate([r['y'][npt:].reshape(cfg.SPC, CH, cfg.D) for r in res], 0)

    def gather(name, shape_tail, prompt):
        sl = slice(0, cfg.PPC) if prompt else slice(cfg.PPC, cfg.NSEQ)
        return np.concatenate([r[name][:, sl] for r in res], 1).reshape((L, -1) + shape_tail)
    outs = [yp, ys]
    for prompt in (True, False):
        outs += [gather('o_swa_k', (128, cfg.SWA_KV, 64), prompt), gather('o_swa_v', (128, cfg.SWA_KV, 64), prompt),
                 gather('o_gla', (cfg.GLA_H, 256, 512), prompt), gather('o_gdn', (cfg.GDN_H, 128, 128), prompt),
                 gather('o_conv', (3, cfg.CONVC), prompt)]
    return tuple(np.ascontiguousarray(o.astype(np.float32)) for o in outs)


def kernel(**inputs):
    return run(FULL, inputs)
```
